# Optimizing a Trainium2 kernel written in Bass

```python
import jax, jax.numpy as jnp
from jax import lax
import numpy as np

D_MODEL = 1024
BATCH = 8
SEQ = 2048
DEPTH = 4

MEM_LEN = 256
EPS = 1e-6
ATTN_HEADS = 4
ATTN_HEAD_DIM = 64
ATTN_WIDTH = ATTN_HEADS * ATTN_HEAD_DIM
DILATED_PATTERNS = ((128, 1), (512, 4), (2048, 16))
WINDOW_BLOCK = 128
ROPE_THETA = 500000.0
ROPE_DIM = ATTN_HEAD_DIM // 4
CONV_GROUPS = 4
CONV_WIDTH = D_MODEL // 4
CONV_K = 3
GDN_HEADS = 4
GDN_HEAD_DIM = 128
GDN_WIDTH = GDN_HEADS * GDN_HEAD_DIM
GDN_CONV_K = 4
GDN_CHUNK = 64
MIX_WIDTH = ATTN_WIDTH + CONV_WIDTH + GDN_WIDTH
IN_SPLITS = (ATTN_WIDTH, ATTN_WIDTH, ATTN_WIDTH,
             CONV_WIDTH, CONV_WIDTH, CONV_WIDTH,
             GDN_WIDTH, GDN_WIDTH, GDN_WIDTH, GDN_HEADS, GDN_HEADS, GDN_WIDTH)
IN_WIDTH = 3 * ATTN_WIDTH + 3 * CONV_WIDTH + 4 * GDN_WIDTH + 2 * GDN_HEADS
XATTN_HEADS = 4
XATTN_HEAD_DIM = D_MODEL // XATTN_HEADS
XATTN_WIDTH = XATTN_HEADS * XATTN_HEAD_DIM
FFN_HIDDEN = -(-8 * D_MODEL // (3 * 256)) * 256

kernel_name = 'hybrid_dilated_conv_deltanet_block'


def rms_norm(x, w):
    x32 = x.astype(jnp.float32)
    y = x32 * lax.rsqrt(jnp.mean(x32 * x32, axis=-1, keepdims=True) + EPS)
    return (y * w.astype(jnp.float32)).astype(x.dtype)


def causal_depthwise_conv(x, w):
    K, C = w.shape
    return lax.conv_general_dilated(x, w[:, None, :].astype(x.dtype), window_strides=(1,),
                                    padding=[(K - 1, 0)], dimension_numbers=('NWC', 'WIO', 'NWC'),
                                    feature_group_count=C)


def rotary_tables(positions):
    inv_freq = jnp.float32(ROPE_THETA) ** (-jnp.arange(0, ROPE_DIM, 2, dtype=jnp.float32) / ROPE_DIM)
    ang = positions.astype(jnp.float32)[..., None] * inv_freq
    return jnp.cos(ang)[:, :, None, :], jnp.sin(ang)[:, :, None, :]


def apply_partial_rotary(x, cos, sin):
    half = ROPE_DIM // 2
    x1 = x[..., :half].astype(jnp.float32)
    x2 = x[..., half:ROPE_DIM].astype(jnp.float32)
    rot = jnp.concatenate([x1 * cos - x2 * sin, x2 * cos + x1 * sin], axis=-1).astype(x.dtype)
    return jnp.concatenate([rot, x[..., ROPE_DIM:]], axis=-1)


def dilated_window_attention(q, k, v, dilation, n_back):
    B, S, H, Dh = q.shape
    QB = WINDOW_BLOCK
    L = S // dilation
    nb = -(-L // QB)
    Lp = nb * QB

    def to_residue(t):
        t = t.reshape(B, L, dilation, H, Dh).transpose(0, 2, 1, 3, 4).reshape(B * dilation, L, H, Dh)
        t = jnp.pad(t, ((0, 0), (0, Lp - L), (0, 0), (0, 0)))
        return t.reshape(B * dilation, nb, QB, H, Dh)

    def with_prev(t):
        prev = jnp.pad(t[:, :-1], ((0, 0), (1, 0), (0, 0), (0, 0), (0, 0)))
        return jnp.concatenate([prev, t], axis=2)

    qb = to_residue(q)
    kw = with_prev(to_residue(k))
    vw = with_prev(to_residue(v))
    s = jnp.einsum('bnqhd,bnkhd->bnhqk', qb, kw, preferred_element_type=jnp.float32)
    qi = jnp.arange(nb)[:, None, None] * QB + jnp.arange(QB)[None, :, None]
    kj = jnp.arange(nb)[:, None, None] * QB - QB + jnp.arange(2 * QB)[None, None, :]
    dist = qi - kj
    mask = (dist >= 0) & (dist <= n_back) & (kj >= 0)
    s = jnp.where(mask[None, :, None], s, -jnp.inf)
    m = jnp.max(s, axis=-1, keepdims=True)
    p = jnp.exp(s - m)
    l = jnp.sum(p, axis=-1, keepdims=True)
    o = jnp.einsum('bnhqk,bnkhd->bnqhd', (p / l).astype(v.dtype), vw)
    lse = (m + jnp.log(l))[..., 0]
    o = o.reshape(B, dilation, Lp, H, Dh)[:, :, :L].transpose(0, 2, 1, 3, 4).reshape(B, S, H, Dh)
    lse = lse.transpose(0, 1, 3, 2).reshape(B, dilation, Lp, H)[:, :, :L]
    lse = lse.transpose(0, 2, 1, 3).reshape(B, S, H)
    return o, lse


def dilated_attention_mixer(q, k, v, cos, sin):
    B, S, _ = q.shape
    q = q.reshape(B, S, ATTN_HEADS, ATTN_HEAD_DIM)
    k = k.reshape(B, S, ATTN_HEADS, ATTN_HEAD_DIM)
    v = v.reshape(B, S, ATTN_HEADS, ATTN_HEAD_DIM)
    q = apply_partial_rotary(q, cos, sin) * (ATTN_HEAD_DIM ** -0.5)
    k = apply_partial_rotary(k, cos, sin)
    outs, lses = [], []
    for window, dilation in DILATED_PATTERNS:
        o, lse = dilated_window_attention(q, k, v, dilation, window // dilation)
        outs.append(o.astype(jnp.float32))
        lses.append(lse)
    wts = jax.nn.softmax(jnp.stack(lses, axis=0), axis=0)
    o = jnp.einsum('pbsh,pbshd->bshd', wts, jnp.stack(outs, axis=0))
    return o.reshape(B, S, ATTN_WIDTH).astype(q.dtype)


def short_conv_mixer(b_gate, c_gate, xv, conv_w):
    return b_gate * causal_depthwise_conv(c_gate * xv, conv_w)


def gated_delta_rule(q, k, v, g, beta):
    B, S, H, Dk = q.shape
    Dv = v.shape[-1]
    C = GDN_CHUNK
    N = S // C

    def chunks(t):
        return jnp.moveaxis(t.reshape(B, N, C, H, *t.shape[3:]), 3, 1)

    q, k, v, g, beta = chunks(q), chunks(k), chunks(v), chunks(g), chunks(beta)
    decay = jnp.cumsum(g, axis=-1)
    causal = jnp.tril(jnp.ones((C, C), dtype=bool))
    strict = jnp.tril(jnp.ones((C, C), dtype=bool), -1)
    rel = jnp.exp(jnp.where(causal, decay[..., :, None] - decay[..., None, :], -jnp.inf))
    k_beta = k * beta[..., None]
    a = jnp.where(strict, jnp.einsum('bhnik,bhnjk->bhnij', k_beta, k) * rel, 0.0)
    eye = jnp.broadcast_to(jnp.eye(C, dtype=jnp.float32), a.shape)
    rhs = jnp.concatenate([v * beta[..., None], k_beta * jnp.exp(decay)[..., None]], axis=-1)
    sol = lax.linalg.triangular_solve(eye + a, rhs, left_side=True, lower=True, unit_diagonal=True)
    u, w = sol[..., :Dv], sol[..., Dv:]
    attn = jnp.where(causal, jnp.einsum('bhnik,bhnjk->bhnij', q, k) * rel, 0.0)
    q_dec = q * jnp.exp(decay)[..., None]
    k_dec = k * jnp.exp(decay[..., -1:] - decay)[..., None]
    chunk_decay = jnp.exp(decay[..., -1])

    def step(state, inp):
        q_i, k_i, u_i, w_i, attn_i, cd_i = inp
        v_new = u_i - jnp.einsum('bhck,bhkv->bhcv', w_i, state)
        o_i = (jnp.einsum('bhck,bhkv->bhcv', q_i, state)
               + jnp.einsum('bhij,bhjv->bhiv', attn_i, v_new))
        state = state * cd_i[..., None, None] + jnp.einsum('bhck,bhcv->bhkv', k_i, v_new)
        return state, o_i

    xs = tuple(jnp.moveaxis(t, 2, 0) for t in (q_dec, k_dec, u, w, attn, chunk_decay))
    state0 = jnp.zeros((B, H, Dk, Dv), jnp.float32)
    _, o = lax.scan(step, state0, xs)
    return o.transpose(1, 0, 3, 2, 4).reshape(B, S, H, Dv)


def gated_deltanet_mixer(q, k, v, a, b, gate, conv_w, a_log, dt_bias, norm_w):
    B, S, _ = q.shape
    qkv = jax.nn.silu(causal_depthwise_conv(jnp.concatenate([q, k, v], axis=-1), conv_w))
    q, k, v = jnp.split(qkv.astype(jnp.float32), 3, axis=-1)
    q = q.reshape(B, S, GDN_HEADS, GDN_HEAD_DIM)
    k = k.reshape(B, S, GDN_HEADS, GDN_HEAD_DIM)
    v = v.reshape(B, S, GDN_HEADS, GDN_HEAD_DIM)
    q = q * lax.rsqrt(jnp.sum(q * q, axis=-1, keepdims=True) + EPS) * (GDN_HEAD_DIM ** -0.5)
    k = k * lax.rsqrt(jnp.sum(k * k, axis=-1, keepdims=True) + EPS)
    g = -jnp.exp(a_log.astype(jnp.float32)) * jax.nn.softplus(a.astype(jnp.float32) + dt_bias.astype(jnp.float32))
    beta = jax.nn.sigmoid(b.astype(jnp.float32))
    o = gated_delta_rule(q, k, v, g, beta)
    gate = jax.nn.silu(gate.astype(jnp.float32)).reshape(B, S, GDN_HEADS, GDN_HEAD_DIM)
    o = rms_norm(o, norm_w) * gate
    return o.reshape(B, S, GDN_WIDTH).astype(gate.dtype if gate.dtype == q.dtype else gate.dtype)


def memory_cross_attention(h, mem, w_q, w_kv, w_o):
    B, S, _ = h.shape
    M = mem.shape[1]
    q = (h @ w_q).reshape(B, S, XATTN_HEADS, XATTN_HEAD_DIM)
    k, v = jnp.split(mem @ w_kv, 2, axis=-1)
    k = k.reshape(B, M, XATTN_HEADS, XATTN_HEAD_DIM)
    v = v.reshape(B, M, XATTN_HEADS, XATTN_HEAD_DIM)
    s = jnp.einsum('bshd,bmhd->bhsm', q, k, preferred_element_type=jnp.float32) * (XATTN_HEAD_DIM ** -0.5)
    p = jax.nn.softmax(s, axis=-1)
    o = jnp.einsum('bhsm,bmhd->bshd', p.astype(v.dtype), v).reshape(B, S, XATTN_WIDTH)
    return o @ w_o


def swiglu_ffn(h, w_gate_up, w_down):
    gate, up = jnp.split(h @ w_gate_up, 2, axis=-1)
    return (jax.nn.silu(gate) * up) @ w_down


def setup_inputs(seed: int = 0) -> dict:
    key = jax.random.key(seed)
    ks = jax.random.split(key, 24)
    f32 = jnp.float32

    def dense(k, shape, fan_in):
        return jax.random.normal(k, shape, f32) * (fan_in ** -0.5)

    def gain(k, shape):
        return 1.0 + 0.02 * jax.random.normal(k, shape, f32)

    x = jax.random.normal(ks[0], (BATCH, SEQ, D_MODEL), f32)
    mem = jax.random.normal(ks[1], (BATCH, MEM_LEN, D_MODEL), f32)
    positions = (jax.random.randint(ks[2], (BATCH, 1), 0, 4096, dtype=jnp.int32)
                 + jnp.arange(SEQ, dtype=jnp.int32)[None, :])
    dt = jnp.exp(jax.random.uniform(ks[3], (DEPTH, GDN_HEADS), f32, np.log(1e-3), np.log(1e-1)))
    return {
        'x': x,
        'mem': mem,
        'positions': positions,
        'norm_mix_pre': gain(ks[4], (DEPTH, D_MODEL)),
        'norm_mix_post': gain(ks[5], (DEPTH, D_MODEL)),
        'w_in': dense(ks[6], (DEPTH, D_MODEL, IN_WIDTH), D_MODEL),
        'conv_short': dense(ks[7], (DEPTH, CONV_K, CONV_WIDTH), CONV_K),
        'conv_gdn': dense(ks[8], (DEPTH, GDN_CONV_K, 3 * GDN_WIDTH), GDN_CONV_K),
        'gdn_a_log': jnp.log(jax.random.uniform(ks[9], (DEPTH, GDN_HEADS), f32, 1.0, 16.0)),
        'gdn_dt_bias': dt + jnp.log(-jnp.expm1(-dt)),
        'gdn_norm': gain(ks[10], (DEPTH, GDN_HEAD_DIM)),
        'w_out': dense(ks[11], (DEPTH, MIX_WIDTH, D_MODEL), MIX_WIDTH),
        'norm_mem': gain(ks[12], (DEPTH, D_MODEL)),
        'norm_xattn_pre': gain(ks[13], (DEPTH, D_MODEL)),
        'norm_xattn_post': gain(ks[14], (DEPTH, D_MODEL)),
        'w_xq': dense(ks[15], (DEPTH, D_MODEL, XATTN_WIDTH), D_MODEL),
        'w_xkv': dense(ks[16], (DEPTH, D_MODEL, 2 * XATTN_WIDTH), D_MODEL),
        'w_xo': dense(ks[17], (DEPTH, XATTN_WIDTH, D_MODEL), XATTN_WIDTH),
        'norm_ffn_pre': gain(ks[18], (DEPTH, D_MODEL)),
        'norm_ffn_post': gain(ks[19], (DEPTH, D_MODEL)),
        'w_gate_up': dense(ks[20], (DEPTH, D_MODEL, 2 * FFN_HIDDEN), D_MODEL),
        'w_down': dense(ks[21], (DEPTH, FFN_HIDDEN, D_MODEL), FFN_HIDDEN),
    }


def reference(x, mem, positions, norm_mix_pre, norm_mix_post, w_in, conv_short, conv_gdn,
              gdn_a_log, gdn_dt_bias, gdn_norm, w_out, norm_mem, norm_xattn_pre, norm_xattn_post,
              w_xq, w_xkv, w_xo, norm_ffn_pre, norm_ffn_post, w_gate_up, w_down):
    cos, sin = rotary_tables(positions)
    split_idx = [int(i) for i in np.cumsum(IN_SPLITS)[:-1]]
    h = x
    for l in range(DEPTH):
        hn = rms_norm(h, norm_mix_pre[l])
        proj = hn @ w_in[l]
        (aq, ak, av, cb, cc, cx, gq, gk, gv, ga, gb, gg) = jnp.split(proj, split_idx, axis=-1)
        y_attn = dilated_attention_mixer(aq, ak, av, cos, sin)
        y_conv = short_conv_mixer(cb, cc, cx, conv_short[l])
        y_gdn = gated_deltanet_mixer(gq, gk, gv, ga, gb, gg, conv_gdn[l],
                                     gdn_a_log[l], gdn_dt_bias[l], gdn_norm[l]).astype(proj.dtype)
        mix = jnp.concatenate([y_attn.astype(proj.dtype), y_conv, y_gdn], axis=-1) @ w_out[l]
        h = h + rms_norm(mix, norm_mix_post[l])
        hn = rms_norm(h, norm_xattn_pre[l])
        xa = memory_cross_attention(hn, rms_norm(mem, norm_mem[l]), w_xq[l], w_xkv[l], w_xo[l])
        h = h + rms_norm(xa, norm_xattn_post[l])
        hn = rms_norm(h, norm_ffn_pre[l])
        h = h + rms_norm(swiglu_ffn(hn, w_gate_up[l], w_down[l]), norm_ffn_post[l])
    return h
```

```python
import numpy as np
import concourse.bass as bass
import concourse.mybir as mybir
from concourse.bass_utils import run_bass_kernel_spmd

F32 = mybir.dt.float32
BF16 = mybir.dt.bfloat16
I32 = mybir.dt.int32
AF = mybir.ActivationFunctionType
ALU = mybir.AluOpType
AX = mybir.AxisListType

L = 4
S = 2048
D = 1024
NT = 16
MEM = 256
FH = 2816
NWT = 29
EPS = 1e-6
N_DMA_SEMS = 24
POOL_ALT = "dve"
MAGIC = 12582912.0
TWO_PI = 2.0 * np.pi
C1 = 6.28125
C2 = 0.0019353071693331003
C3 = 1.0253131677018246e-11


class Prog:
    def __init__(self, nc):
        self.nc = nc
        self.eng = {"pe": nc.tensor, "act": nc.scalar, "dve": nc.vector,
                    "pool": nc.gpsimd, "sp": nc.sync}
        self.ops = []
        self.last_w = {}
        self.readers = {}
        self.children = {}
        self.dma_rr = 0
        self.dma_rr_sw = 0
        self.last_on_dma_sem = [None] * N_DMA_SEMS
        self.last_eng = {}

    def _related(self, key):
        out = []
        for i in range(1, len(key) + 1):
            pk = key[:i]
            if pk in self.last_w or pk in self.readers:
                out.append(pk)
        for ck in self.children.get(key, ()):
            if ck != key:
                out.append(ck)
        return out

    def _register(self, key):
        if key in self.last_w or key in self.readers:
            return
        for i in range(1, len(key)):
            self.children.setdefault(key[:i], set()).add(key)

    def add(self, eng, fn, reads=(), writes=(), dma=False, prefetch=False, extra_deps=()):
        deps = set(extra_deps)
        reads = [k if isinstance(k, tuple) else (k,) for k in reads]
        writes = [k if isinstance(k, tuple) else (k,) for k in writes]
        for k in reads:
            for rk in self._related(k):
                if rk in self.last_w:
                    deps.add(self.last_w[rk])
        for k in writes:
            for rk in self._related(k):
                if rk in self.last_w:
                    deps.add(self.last_w[rk])
                for r in self.readers.get(rk, {}).values():
                    deps.add(r)
        oid = len(self.ops)
        op = dict(eng=eng, dma=dma, prefetch=prefetch)
        if dma:
            if eng == "pool":
                s = 16 + self.dma_rr_sw % (N_DMA_SEMS - 16)
                self.dma_rr_sw += 1
            else:
                s = self.dma_rr % 16
                self.dma_rr += 1
            op["dsem"] = s
            if self.last_on_dma_sem[s] is not None:
                deps.add(self.last_on_dma_sem[s])
            self.last_on_dma_sem[s] = oid
        self.ops.append(op)
        for k in reads:
            self._register(k)
            self.readers.setdefault(k, {})[("dma", oid) if dma else eng] = oid
        for k in writes:
            self._register(k)
            for ck in list(self.children.get(k, ())):
                self.last_w.pop(ck, None)
                self.readers.pop(ck, None)
            self.last_w[k] = oid
            self.readers[k] = {}
        deps.discard(oid)
        if not dma:
            self.last_eng[eng] = oid
        self._init_sems()
        engine = self.eng[eng]
        req = {}
        for d in deps:
            od = self.ops[d]
            if od["eng"] == eng and not od["dma"] and eng in ("pe", "sp"):
                continue
            sem, val = od["sig"]
            if req.get(sem.num, (None, 0))[1] < val:
                req[sem.num] = (sem, val)
        for sem, val in req.values():
            if self.waited.get((eng, sem.num), 0) >= val:
                continue
            engine.wait_ge(sem, val)
            self.waited[(eng, sem.num)] = val
            self.n_wait += 1
        ins = fn(engine)
        if dma:
            s = op["dsem"]
            self.dcount[s] += 16
            ins.then_inc(self.dsem[s], 16)
            op["sig"] = (self.dsem[s], self.dcount[s])
        else:
            self.ecount[eng] += 1
            ins.then_inc(self.esem[eng], 1)
            op["sig"] = (self.esem[eng], self.ecount[eng])
        return oid

    def _init_sems(self):
        if hasattr(self, "esem"):
            return
        nc = self.nc
        self.esem = {e: nc.alloc_semaphore("s_" + e) for e in ("pe", "act", "dve", "pool", "sp")}
        self.dsem = [nc.alloc_semaphore("d_%d" % i) for i in range(N_DMA_SEMS)]
        self.ecount = {e: 0 for e in self.esem}
        self.dcount = [0] * N_DMA_SEMS
        self.waited = {}
        self.n_wait = 0

    def barrier(self):
        deps = set(self.last_eng.values())
        for o in self.last_on_dma_sem:
            if o is not None and not self.ops[o]["prefetch"]:
                deps.add(o)
        ids = []
        for e in ("pe", "act", "dve", "pool", "sp"):
            ids.append(self.add(e, lambda en: en.nop(), extra_deps=deps))
        return ids

    def emit(self, final_wait_ops=()):
        deps = set(self.last_eng.values())
        for o in self.last_on_dma_sem:
            if o is not None:
                deps.add(o)
        deps.update(final_wait_ops)
        for e in ("pe", "act", "dve", "pool", "sp"):
            self.add(e, lambda en: en.nop(), extra_deps=deps)
        for d in final_wait_ops:
            sem, val = self.ops[d]["sig"]
            self.nc.sync.wait_ge(sem, val)
        self.stats = dict(n_ops=len(self.ops), n_wait=self.n_wait, counts=dict(self.ecount))
        return self.stats


AW = 43520
KB = 256
OFF_EPI = 37888


def build(nlayers=L, debug=None, stop=None, gstop=None):
    nc = bass.Bass("TRN2", target_bir_lowering=False)
    P = Prog(nc)
    dbg_outs = {}

    x_d = nc.dram_tensor("x", [S, D], F32, kind="ExternalInput").ap()
    mem_d = nc.dram_tensor("mem", [MEM, D], F32, kind="ExternalInput").ap()
    pos_d = nc.dram_tensor("pos", [1, S], I32, kind="ExternalInput").ap()
    W_d = nc.dram_tensor("W", [nlayers * NWT, 128, 4096], F32, kind="ExternalInput").ap()
    WD_d = nc.dram_tensor("WD", [max(nlayers, 1), 128, 22 * 1024], F32, kind="ExternalInput").ap()
    WAB_d = nc.dram_tensor("WAB", [L, 128, 64], F32, kind="ExternalInput").ap()
    NORMS_d = nc.dram_tensor("NORMS", [L * 7, D], F32, kind="ExternalInput").ap()
    PSM_d = nc.dram_tensor("PSM", [128, L * 54], F32, kind="ExternalInput").ap()
    GDNP_d = nc.dram_tensor("GDNP", [L, 136], F32, kind="ExternalInput").ap()
    CONST_d = nc.dram_tensor("CONST", [128, 1154], F32, kind="ExternalInput").ap()
    out_d = nc.dram_tensor("out", [S, D], F32, kind="ExternalOutput").ap()
    h_d = nc.dram_tensor("h_scr", [S, D], F32).ap()

    def sb(name, shape, dt):
        return nc.alloc_sbuf_tensor(name, shape, dt)

    cst = sb("cst", [128, 1154], F32)
    identf = cst[:, 0:128]
    CM = cst[:, 128:256]
    maskbigT = cst[:, 256:384]
    offdiag = cst[:, 384:512]
    onesf = cst[:, 512:640]
    invf = cst[:, 1152:1153]
    sgn = cst[:, 1153:1154]
    identb = sb("identb", [128, 128], BF16)
    onesb = sb("onesb", [128, 128], BF16)
    mask2 = sb("mask2", [128, 2, 2, 128], BF16)
    ropeC = sb("ropeC", [128, S], F32)
    ropeS = sb("ropeS", [128, S], F32)
    normw = sb("normw", [128, 3, D], F32)
    psm = sb("psm", [128, L * 54], F32)
    gdnp = sb("gdnp", [128, 136], F32)
    stat = sb("stat", [128, 64], F32)
    AR = sb("arena", [128, AW], F32)
    PS = nc.alloc_psum_tensor("ps", [128, 8, 512], F32)

    def vw(off, n, dt=F32, **shape):
        v = AR[:, off:off + n]
        if dt != F32:
            v = v.bitcast(dt)
        return v

    def v3(off, a, b, dt):
        n = a * b * (2 if dt == BF16 else 4) // 4
        return vw(off, n, dt).rearrange("p (a b) -> p a b", a=a)

    def dma(q, out, in_, r, w, prefetch=False):
        return P.add(q, lambda e: e.dma_start(out=out, in_=in_), reads=r, writes=w, dma=True,
                     prefetch=prefetch)

    def act(out, in_, func, r, w, bias=None, scale=None, accum=None):
        kw = {}
        if bias is not None:
            kw["bias"] = bias
        if scale is not None:
            kw["scale"] = scale
        if accum is not None:
            kw["accum_out"] = accum
        return P.add("act", lambda e: e.activation(out=out, in_=in_, func=func, **kw), reads=r, writes=w)

    def cp(eng, out, in_, r, w):
        if eng == "act":
            return P.add("act", lambda e: e.copy(out=out, in_=in_), reads=r, writes=w)
        return P.add(eng, lambda e: e.tensor_copy(out=out, in_=in_), reads=r, writes=w)

    def tt(eng, out, in0, in1, op, r, w):
        if eng == "pool":
            eng = POOL_ALT
        return P.add(eng, lambda e: e.tensor_tensor(out=out, in0=in0, in1=in1, op=op), reads=r, writes=w)

    def ts(eng, out, in0, s1, op0, r, w, s2=None, op1=None):
        if op1 is None:
            return P.add(eng, lambda e: e.tensor_scalar(out=out, in0=in0, scalar1=s1, scalar2=None, op0=op0),
                         reads=r, writes=w)
        return P.add(eng, lambda e: e.tensor_scalar(out=out, in0=in0, scalar1=s1, scalar2=s2, op0=op0, op1=op1),
                     reads=r, writes=w)

    def stt(eng, out, in0, scalar, in1, op0, op1, r, w):
        return P.add(eng, lambda e: e.scalar_tensor_tensor(out=out, in0=in0, scalar=scalar, in1=in1,
                                                           op0=op0, op1=op1), reads=r, writes=w)

    def pe(fn, r, w):
        return P.add("pe", fn, reads=r, writes=w)

    bank_rr = [0]
    pair_rr = [0]

    def bank():
        b = bank_rr[0] % 8
        bank_rr[0] += 1
        return b

    def bank2():
        b = (pair_rr[0] % 4) * 2
        pair_rr[0] += 1
        return b

    def psb(b):
        return PS[:, b, :]

    def psb_bf(b, a):
        return PS[:, b, :].bitcast(BF16).rearrange("p (a b) -> p a b", a=a)

    def dump(name, ap, shape, r):
        if debug is None or name not in debug:
            return
        t = nc.dram_tensor("dbg_" + name, list(shape), ap.dtype, kind="ExternalOutput").ap()
        dbg_outs[name] = t
        debug[name] = dma("sp", t, ap, r, [("dbg", name)])

    dma("sp", cst[:], CONST_d[:, :], [], ["cst"])
    dma("sp", psm[:], PSM_d[:, :], [], ["psm"])
    cp("dve", identb[:], identf, ["cst"], ["identb"])
    cp("dve", onesb[:], onesf, ["cst"], ["onesb"])
    cp("dve", mask2[:, 0, :, :], cst[:, 640:896].rearrange("p (a b) -> p a b", a=2), ["cst"], ["mask2"])
    cp("dve", mask2[:, 1, :, :], cst[:, 640:896].rearrange("p (a b) -> p a b", a=2), ["cst"], ["mask2"])

    def rope_tables():
        posi = vw(0, S, I32)
        ang = vw(2048, S)
        t1 = vw(4096, S)
        t2 = vw(6144, S)
        dma("sp", posi, pos_d[0:1, :].partition_broadcast(128), [], ["r_posi"])
        cp("dve", ang, posi, ["r_posi"], ["r_ang"])
        ts("dve", ang, ang, invf, ALU.mult, ["r_ang", "cst"], ["r_ang"])
        ts("dve", t1, ang, 1.0 / TWO_PI, ALU.mult, ["r_ang"], ["r_t1"], s2=MAGIC, op1=ALU.add)
        ts("dve", t1, t1, MAGIC, ALU.subtract, ["r_t1"], ["r_t1"])
        stt("dve", ang, t1, -C1, ang, ALU.mult, ALU.add, ["r_t1", "r_ang"], ["r_ang"])
        stt("dve", ang, t1, -C2, ang, ALU.mult, ALU.add, ["r_t1", "r_ang"], ["r_ang"])
        stt("dve", ang, t1, -C3, ang, ALU.mult, ALU.add, ["r_t1", "r_ang"], ["r_ang"])
        ts("dve", t1, ang, float(np.pi), ALU.is_gt, ["r_ang"], ["r_t1"])
        stt("dve", ang, t1, -TWO_PI, ang, ALU.mult, ALU.add, ["r_t1", "r_ang"], ["r_ang"])
        ts("dve", t1, ang, -float(np.pi), ALU.is_lt, ["r_ang"], ["r_t1"])
        stt("dve", ang, t1, TWO_PI, ang, ALU.mult, ALU.add, ["r_t1", "r_ang"], ["r_ang"])
        ts("dve", ang, ang, float(np.pi), ALU.min, ["r_ang"], ["r_ang"], s2=-float(np.pi), op1=ALU.max)
        act(t2, ang, AF.Sin, ["r_ang"], ["r_t2"])
        ts("dve", ropeS[:], t2, sgn, ALU.mult, ["r_t2", "cst"], ["ropeS"])
        stt("dve", t1, ang, -1.0, ang, ALU.mult, ALU.max, ["r_ang"], ["r_t1"])
        ts("dve", t1, t1, -1.0, ALU.mult, ["r_t1"], ["r_t1"], s2=float(np.pi / 2), op1=ALU.add)
        act(ropeC[:], t1, AF.Sin, ["r_t1"], ["ropeC"])

    rope_tables()
    dump("ropeC", ropeC[:], [128, S], ["ropeC"])
    dump("ropeS", ropeS[:], [128, S], ["ropeS"])
    if stop != "R0":
        P.barrier()
    if stop in ("R", "R0"):
        nlayers = 0

    nw_rr = [0]

    def load_normw(l, which):
        slot = nw_rr[0] % 3
        nw_rr[0] += 1
        row = l * 7 + which
        dma("sp", normw[:, slot, :], NORMS_d[row:row + 1, :].partition_broadcast(128), [], [("normw", slot)])
        return slot

    wa_state = dict(rr=0, n=2, base=16384)

    def set_wa(base, n):
        wa_state.update(base=base, n=n, rr=0)

    def load_w(l, ti):
        slot = wa_state["rr"] % wa_state["n"]
        wa_state["rr"] += 1
        off = wa_state["base"] + slot * 2048
        v = vw(off, 2048, BF16)
        dma("pool", v, W_d[l * NWT + ti], [], [("WA", off)], prefetch=True)
        return v.rearrange("p (k n) -> p k n", k=8), ("WA", off)

    e_ht = [vw(OFF_EPI, 1024), vw(OFF_EPI + 1024, 1024)]
    e_yn = vw(OFF_EPI + 2048, 1024)
    e_hn = vw(OFF_EPI + 3072, 512, BF16)
    e_junk = vw(OFF_EPI + 3584, 512, BF16)
    epi_rr = [0]

    def prenorm_transpose(hsrc_ap, hkey, wslot, t, xT, xkey):
        c = 32 + (epi_rr[0] % 4) * 2
        act(e_junk, hsrc_ap, AF.Square, [hkey], ["e_junk"], accum=stat[:, c:c + 1])
        P.add("act", lambda e: e.activation(out=stat[:, c + 1:c + 2], in_=stat[:, c:c + 1], func=AF.Sqrt,
                                            bias=EPS, scale=1.0 / D), reads=["e_junk"], writes=[("stat", c)])
        P.add("dve", lambda e: e.reciprocal(out=stat[:, c + 1:c + 2], in_=stat[:, c + 1:c + 2]),
              reads=[("stat", c)], writes=[("stat", c)])
        stt("dve", e_hn, hsrc_ap, stat[:, c + 1:c + 2], normw[:, wslot, :], ALU.mult, ALU.mult,
            [hkey, ("stat", c), ("normw", wslot)], ["e_hn"])
        b = bank()
        pt = psb_bf(b, 8)

        def tr(e):
            ins = None
            for kc in range(8):
                ins = e.transpose(pt[:, kc, :], e_hn[:, kc * 128:(kc + 1) * 128], identb[:])
            return ins
        pe(tr, ["e_hn", "identb"], [("ps", b)])
        cp("act", xT[:, :, t * 128:(t + 1) * 128], pt, [("ps", b)], [(xkey, t)])

    def epilogue(t, pb, src_d, dst_d, wpost, wnext, xT_next, xkey_next):
        i = epi_rr[0]
        epi_rr[0] += 1
        ht = e_ht[i % 2]
        hk = ("e_ht", i % 2)
        c = 40 + (i % 4) * 2
        ypsum = PS[:, pb:pb + 2, :]
        dma("sp", ht, src_d[t * 128:(t + 1) * 128, :], [("hd", t)], [hk])
        act(e_junk.rearrange("p (a b) -> p a b", a=2), ypsum, AF.Square, [("ps", pb), ("ps", pb + 1)],
            ["e_junk"], accum=stat[:, c:c + 1])
        P.add("act", lambda e: e.activation(out=stat[:, c + 1:c + 2], in_=stat[:, c:c + 1], func=AF.Sqrt,
                                            bias=EPS, scale=1.0 / D), reads=["e_junk"], writes=[("stat", c)])
        P.add("dve", lambda e: e.reciprocal(out=stat[:, c + 1:c + 2], in_=stat[:, c + 1:c + 2]),
              reads=[("stat", c)], writes=[("stat", c)])
        stt("dve", e_yn.rearrange("p (a b) -> p a b", a=2), ypsum, stat[:, c + 1:c + 2],
            normw[:, wpost, :].rearrange("p (a b) -> p a b", a=2), ALU.mult, ALU.mult,
            [("ps", pb), ("ps", pb + 1), ("stat", c), ("normw", wpost)], ["e_yn"])
        tt("pool", ht, ht, e_yn, ALU.add, [hk, "e_yn"], [hk])
        last = dma("sp", dst_d[t * 128:(t + 1) * 128, :], ht, [hk], [("hd", t)] if dst_d is h_d else [("outd", t)])
        if wnext is not None:
            prenorm_transpose(ht, hk, wnext, t, xT_next, xkey_next)
        return last

    def mm_feat(wv, wkey, j, xT, xkey, tok0, ntok, b, ncol=128):
        def f(e):
            ins = None
            for kc in range(8):
                ins = e.matmul(PS[:, b, 0:ntok], lhsT=wv[:, kc, j * 128:j * 128 + ncol],
                               rhs=xT[:, kc, tok0:tok0 + ntok], start=(kc == 0), stop=(kc == 7))
            return ins
        pe(f, [wkey, xkey], [("ps", b)])

    final_ops = []
    xT0 = v3(0, 8, S, BF16)
    xTF = v3(22528, 8, S, BF16)

    for l in range(nlayers):
        src_h = x_d if l == 0 else h_d
        w_pre = load_normw(l, 0)
        for t in range(NT):
            ht = e_ht[t % 2]
            hk = ("e_ht", t % 2)
            dma("sp", ht, src_h[t * 128:(t + 1) * 128, :], [("hd", t)], [hk])
            prenorm_transpose(ht, hk, w_pre, t, xT0, "xT")
            epi_rr[0] += 1
        if l == 0:
            dump("xT", xT0, [128, 8, S], ["xT"])
        if stop == "M0":
            break

        mixT = v3(8192, 8, S, BF16)
        set_wa(16384, 2)
        TMPA = 20480
        aqT = v3(25088, 2, S, BF16)
        akT = v3(25088 + 2048, 2, S, BF16)
        avT = v3(25088 + 4096, 2, S, BF16)
        gqT = v3(31232, 4, S, BF16)
        gkT = v3(31232 + 4096, 4, S, BF16)
        gvT = v3(31232 + 8192, 4, S, BF16)
        pc = l * 54

        dma("sp", gdnp[:], GDNP_d[l:l + 1, :].partition_broadcast(128), [], ["gdnp"])
        wab32 = vw(TMPA + 3100, 64)
        wabv = vw(TMPA + 3164, 32, BF16).rearrange("p (k n) -> p k n", k=8)
        dma("sp", wab32, WAB_d[l], [], ["wab32"])
        cp("dve", wabv, wab32.rearrange("p (k n) -> p k n", k=8), ["wab32"], ["wab"])

        u_c = vw(TMPA, 2050)
        tmpx = vw(TMPA + 2050, 512)
        acc0 = vw(TMPA + 2562, 512)
        for c in range(2):
            wv, wk = load_w(l, c)
            P.add("dve", lambda e: e.memset(u_c[:, 0:2], 0.0), reads=[], writes=[("u_c", "pad")])
            for tt_ in range(4):
                tok0 = tt_ * 512
                bx, bc, bb, bv = bank(), bank(), bank(), bank()
                mm_feat(wv, wk, 0, xT0, "xT", tok0, 512, bx)
                mm_feat(wv, wk, 1, xT0, "xT", tok0, 512, bc)
                mm_feat(wv, wk, 2, xT0, "xT", tok0, 512, bb)
                mm_feat(wv, wk, 3, xT0, "xT", tok0, 512, bv)
                cp("act", tmpx, psb(bx), [("ps", bx)], ["tmpx"])
                tt("dve", u_c[:, 2 + tok0:2 + tok0 + 512], tmpx, psb(bc), ALU.mult,
                   ["tmpx", ("ps", bc)], [("u_c", tt_)])
                rk = [("u_c", tt_), ("u_c", tt_ - 1) if tt_ > 0 else ("u_c", "pad"), "psm"]
                ts("dve", acc0, u_c[:, tok0:tok0 + 512], psm[:, pc + c * 3:pc + c * 3 + 1], ALU.mult,
                   rk, ["acc0"])
                stt("dve", acc0, u_c[:, tok0 + 1:tok0 + 513], psm[:, pc + c * 3 + 1:pc + c * 3 + 2], acc0,
                    ALU.mult, ALU.add, rk + ["acc0"], ["acc0"])
                stt("dve", acc0, u_c[:, tok0 + 2:tok0 + 514], psm[:, pc + c * 3 + 2:pc + c * 3 + 3], acc0,
                    ALU.mult, ALU.add, rk + ["acc0"], ["acc0"])
                tt("dve", mixT[:, 2 + c, tok0:tok0 + 512], acc0, psb(bb), ALU.mult, ["acc0", ("ps", bb)],
                   [("mixT", 2 + c, tt_)])
                cp("act", avT[:, c, tok0:tok0 + 512], psb(bv), [("ps", bv)], [("avT", c, tt_)])
        P.barrier()
        if stop == "A1a":
            break

        SM = TMPA + 4096
        ab_tok = vw(SM, 128).rearrange("p (t n) -> p t n", t=16)
        g_tok = vw(SM + 128, 64).rearrange("p (t n) -> p t n", t=16)
        beta = vw(SM + 192, 64).rearrange("p (t n) -> p t n", t=16)
        nbeta = vw(SM + 256, 64).rearrange("p (t n) -> p t n", t=16)
        negdec = vw(SM + 320, 64).rearrange("p (t n) -> p t n", t=16)
        edl = vw(SM + 384, 64).rearrange("p (t n) -> p t n", t=16)
        cdb = vw(SM + 448, 64).rearrange("p (t n) -> p t n", t=16)
        b = bank()
        abp = PS[:, b, 0:128].rearrange("p (t n) -> p t n", t=16)

        def abmm(e):
            ins = None
            for t in range(NT):
                for kc in range(8):
                    ins = e.matmul(abp[:, t, :], lhsT=xT0[:, kc, t * 128:(t + 1) * 128], rhs=wabv[:, kc, :],
                                   start=(kc == 0), stop=(kc == 7))
            return ins
        pe(abmm, ["xT", "wab"], [("ps", b)])
        cp("act", ab_tok, abp, [("ps", b)], ["ab_tok"])
        dtb = gdnp[:, 132:136].unsqueeze(1).to_broadcast([128, 16, 4])
        alg = gdnp[:, 128:132].unsqueeze(1).to_broadcast([128, 16, 4])
        tt("dve", g_tok, ab_tok[:, :, 0:4], dtb, ALU.add, ["ab_tok", "gdnp"], ["g_tok"])
        act(g_tok, g_tok, AF.Exp, ["g_tok"], ["g_tok"])
        act(g_tok, g_tok, AF.Ln, ["g_tok"], ["g_tok"], bias=1.0)
        act(nbeta, alg, AF.Exp, ["gdnp"], ["nbeta"])
        stt("dve", g_tok, g_tok, -1.0, nbeta, ALU.mult, ALU.mult, ["g_tok", "nbeta"], ["g_tok"])
        act(beta, ab_tok[:, :, 4:8], AF.Sigmoid, ["ab_tok"], ["beta"])
        ts("dve", nbeta, beta, -1.0, ALU.mult, ["beta"], ["nbeta"])
        b = bank()
        g64 = vw(SM + 128, 64)
        pe(lambda e: e.matmul(PS[:, b, 0:64], lhsT=CM, rhs=g64, start=True, stop=True), ["cst", "g_tok"], [("ps", b)])
        ts("dve", vw(SM + 320, 64), PS[:, b, 0:64], -1.0, ALU.mult, [("ps", b)], ["negdec"])
        b2 = bank()
        pe(lambda e: e.matmul(PS[:, b2, 0:64], lhsT=onesf, rhs=g64, start=True, stop=True), ["cst", "g_tok"], [("ps", b2)])
        act(vw(SM + 448, 64), PS[:, b2, 0:64], AF.Exp, [("ps", b2)], ["cdb"])
        tt("dve", vw(SM + 384, 64), PS[:, b2, 0:64], vw(SM + 320, 64), ALU.add, [("ps", b2), "negdec"], ["edl"])
        act(vw(SM + 384, 64), vw(SM + 384, 64), AF.Exp, ["edl"], ["edl"])

        if stop == "A1b":
            break
        rb = [vw(TMPA, 515), vw(TMPA + 515, 515)]
        gacc = vw(TMPA + 1030, 512)
        gso = vw(TMPA + 1542, 512)
        gsq = vw(TMPA + 2054, 512)
        grn = vw(TMPA + 2566, 512)
        for kind in range(3):
            wv, wk = load_w(l, 2 + kind)
            dstT = (gqT, gkT, gvT)[kind]
            for hh in range(4):
                cg = kind * 4 + hh
                wc = pc + 6 + cg * 4
                for tt_ in range(4):
                    tok0 = tt_ * 512
                    r_ = rb[tt_ % 2]
                    rkk = ("rb", tt_ % 2)
                    b = bank()
                    mm_feat(wv, wk, hh, xT0, "xT", tok0, 512, b)
                    if tt_ == 0:
                        P.add("dve", lambda e, r_=r_: e.memset(r_[:, 0:3], 0.0), reads=[], writes=[rkk])
                    else:
                        cp("dve", r_[:, 0:3], rb[(tt_ - 1) % 2][:, 512:515], [("rb", (tt_ - 1) % 2)], [rkk])
                    cp("act", r_[:, 3:515], psb(b), [("ps", b)], [rkk])
                    ts("dve", gacc, r_[:, 0:512], psm[:, wc:wc + 1], ALU.mult, [rkk, "psm"], ["gacc"])
                    for j in range(1, 4):
                        stt("dve", gacc, r_[:, j:j + 512], psm[:, wc + j:wc + j + 1], gacc, ALU.mult, ALU.add,
                            [rkk, "psm", "gacc"], ["gacc"])
                    if kind == 2:
                        act(dstT[:, hh, tok0:tok0 + 512], gacc, AF.Silu, ["gacc"], [("gT", kind, hh, tt_)])
                    else:
                        act(gso, gacc, AF.Silu, ["gacc"], ["gso"])
                        act(gsq, gso, AF.Square, ["gso"], ["gsq"])
                        b2 = bank()
                        pe(lambda e, b2=b2: e.matmul(PS[:, b2, :], lhsT=onesf, rhs=gsq, start=True, stop=True),
                           ["cst", "gsq"], [("ps", b2)])
                        act(grn, psb(b2), AF.Sqrt, [("ps", b2)], ["grn"], bias=EPS)
                        P.add("dve", lambda e: e.reciprocal(out=grn, in_=grn), reads=["grn"], writes=["grn"])
                        if kind == 0:
                            stt("dve", dstT[:, hh, tok0:tok0 + 512], gso, float(128 ** -0.5), grn, ALU.mult, ALU.mult,
                                ["gso", "grn"], [("gT", kind, hh, tt_)])
                        else:
                            tt("dve", dstT[:, hh, tok0:tok0 + 512], gso, grn, ALU.mult, ["gso", "grn"],
                               [("gT", kind, hh, tt_)])
        if l == 0:
            dump("gqT", gqT, [128, 4, S], ["gT"])
            dump("gkT", gkT, [128, 4, S], ["gT"])
            dump("gvT", gvT, [128, 4, S], ["gT"])
            dump("g_tok", g_tok, [128, 16, 4], ["g_tok"])
            dump("beta", beta, [128, 16, 4], ["beta"])
            dump("negdec", negdec, [128, 16, 4], ["negdec"])
        P.barrier()
        if stop == "A1":
            break

        wgg, wggk = load_w(l, 5)
        G0 = 25088
        f = [vw(G0 + i * 512, 512).rearrange("p (h n) -> p h n", h=4) for i in range(8)]
        gb4, tE, Es, tX, Pm, Sm, osb, sqb = f
        hb = [vw(TMPA + i * 256, 256, BF16).rearrange("p (h n) -> p h n", h=4) for i in range(15)]
        edbc, Xa, Xb, XTa, XTb, Pb, attnT, kdT, qdT, kdec, vtk, Rb, vnew, Sb, ytok = hb
        gst = stat[:, 0:8]
        gnw = gdnp[:, 0:128].unsqueeze(1).to_broadcast([128, 4, 128])
        P.add("dve", lambda e: e.memset(Sm, 0.0), reads=[], writes=["Sm"])
        P.add("dve", lambda e: e.memset(Sb, 0.0), reads=[], writes=["Sb"])

        def mm4(bk, lhs_fn, rhs_fn, r, start=True, stop=True, nb=None):
            def fn(e):
                ins = None
                for h in range(4):
                    ins = e.matmul(PS[:, bk, h * 128:(h + 1) * 128], lhsT=lhs_fn(h), rhs=rhs_fn(h),
                                   start=start, stop=stop)
                return ins
            pe(fn, r, [("ps", bk)])

        def ps4(bk):
            return PS[:, bk, :].rearrange("p (h n) -> p h n", h=4)

        def bc4(ap2):
            return ap2.unsqueeze(2).to_broadcast([128, 4, 128])

        for t_ in range(NT):
            t = (t_ % 2) if gstop == 400 else t_
            tk = slice(t * 128, (t + 1) * 128)
            cp("dve", gb4, bc4(g_tok[:, t, :]), ["g_tok"], ["gb4"])
            bD = bank()
            mm4(bD, lambda h: gb4[:, h, :], lambda h: CM, ["gb4", "cst"])
            act(edbc, ps4(bD), AF.Exp, [("ps", bD)], ["edbc"])
            if gstop == 1:
                break
            tt("dve", tE, ps4(bD), bc4(negdec[:, t, :]), ALU.add, [("ps", bD), "negdec"], ["tE"])
            tt("pool", tE, tE, maskbigT.unsqueeze(1).to_broadcast([128, 4, 128]), ALU.add, ["tE", "cst"], ["tE"])
            act(tE, tE, AF.Exp, ["tE"], ["tE"])
            tt("pool", Es, tE, offdiag.unsqueeze(1).to_broadcast([128, 4, 128]), ALU.mult, ["tE", "cst"], ["Es"])
            if gstop == 2:
                break
            bG = bank()
            mm4(bG, lambda h: gkT[:, h, tk], lambda h: gkT[:, h, tk], ["gT"])
            bQ = bank()
            mm4(bQ, lambda h: gkT[:, h, tk], lambda h: gqT[:, h, tk], ["gT"])
            tt("dve", tX, ps4(bG), Es, ALU.mult, [("ps", bG), "Es"], ["tX"])
            tt("dve", Xa, tX, bc4(nbeta[:, t, :]), ALU.mult, ["tX", "nbeta"], ["Xa"])
            tt("dve", attnT, ps4(bQ), tE, ALU.mult, [("ps", bQ), "tE"], ["attnT"])
            tt("pool", kdT, gkT[:, :, tk], edbc, ALU.mult, ["gT", "edbc"], ["kdT"])
            tt("pool", qdT, gqT[:, :, tk], edbc, ALU.mult, ["gT", "edbc"], ["qdT"])
            if gstop == 3:
                break
            bT = bank()
            ptb = psb_bf(bT, 8)

            def trX(e):
                ins = None
                for h in range(4):
                    ins = e.transpose(ptb[:, h, :], Xa[:, h, :], identb[:])
                return ins
            pe(trX, ["Xa", "identb"], [("ps", bT)])
            Xf = [vw(16384 + i * 512, 512).rearrange("p (h n) -> p h n", h=4) for i in range(4)]
            cp("act", Xf[2], ptb[:, 0:4, :], [("ps", bT)], ["Xf2"])
            cp("act", Xf[0], Xa, ["Xa"], ["Xf0"])
            if gstop == 4:
                break
            tt("dve", Pm, Xf[0], identf.unsqueeze(1).to_broadcast([128, 4, 128]), ALU.add, ["Xf0", "cst"], ["Pm"])
            Xc, XTc, Xn, XTn = Xf[0], Xf[2], Xf[1], Xf[3]
            kX, kXT, kXn, kXTn = "Xf0", "Xf2", "Xf1", "Xf3"
            NLEV = 5
            for k in range(1, (NLEV + 1) if gstop != 200 else 1):
                b3 = bank()
                mm4(b3, lambda h, Xc=Xc: Xc[:, h, :], lambda h, XTc=XTc: XTc[:, h, :], [kX, kXT])
                cp("act", XTn, ps4(b3), [("ps", b3)], [kXTn])
                if k < NLEV:
                    b2_ = bank()
                    mm4(b2_, lambda h, XTc=XTc: XTc[:, h, :], lambda h, Xc=Xc: Xc[:, h, :], [kX, kXT])
                    cp("act", Xn, ps4(b2_), [("ps", b2_)], [kXn])
                b1 = bank()
                mm4(b1, lambda h, XTn=XTn: XTn[:, h, :], lambda h: Pm[:, h, :], [kXTn, "Pm"])
                tt("dve", Pm, Pm, ps4(b1), ALU.add, ["Pm", ("ps", b1)], ["Pm"])
                Xc, XTc, Xn, XTn = Xn, XTn, Xc, XTc
                kX, kXT, kXn, kXTn = kXn, kXTn, kX, kXT
            cp("act", Pb, Pm, ["Pm"], ["Pb"])
            if gstop == 5:
                break
            bK = bank()
            pkb = psb_bf(bK, 8)

            def trKV(e):
                ins = None
                for h in range(4):
                    ins = e.transpose(pkb[:, h, :], gkT[:, h, tk], identb[:])
                for h in range(4):
                    ins = e.transpose(pkb[:, 4 + h, :], gvT[:, h, tk], identb[:])
                return ins
            pe(trKV, ["gT", "identb"], [("ps", bK)])
            tt("dve", kdec, pkb[:, 0:4, :], bc4(edl[:, t, :]), ALU.mult, [("ps", bK), "edl"], ["kdec"])
            cp("act", vtk, pkb[:, 4:8, :], [("ps", bK)], ["vtk"])
            if gstop == 6:
                break
            bGG = bank()

            def ggmm(e):
                ins = None
                for kc in range(8):
                    ins = e.matmul(PS[:, bGG, :], lhsT=xT0[:, kc, tk], rhs=wgg[:, kc, :], start=(kc == 0), stop=(kc == 7))
                return ins
            pe(ggmm, ["xT", wggk], [("ps", bGG)])
            if gstop == 7:
                break
            bA = bank()
            mm4(bA, lambda h: kdT[:, h, :], lambda h: Sb[:, h, :], ["kdT", "Sb"])
            tt("dve", Rb, vtk, ps4(bA), ALU.subtract, ["vtk", ("ps", bA)], ["Rb"])
            bB = bank()
            mm4(bB, lambda h: Pb[:, h, :], lambda h: Rb[:, h, :], ["Pb", "Rb"])
            tt("dve", vnew, ps4(bB), bc4(beta[:, t, :]), ALU.mult, [("ps", bB), "beta"], ["vnew"])
            if gstop == 8:
                break
            bO = bank()

            def omm(e):
                ins = None
                for h in range(4):
                    e.matmul(PS[:, bO, h * 128:(h + 1) * 128], lhsT=qdT[:, h, :], rhs=Sb[:, h, :], start=True, stop=False)
                    ins = e.matmul(PS[:, bO, h * 128:(h + 1) * 128], lhsT=attnT[:, h, :], rhs=vnew[:, h, :],
                                   start=False, stop=True)
                return ins
            pe(omm, ["qdT", "Sb", "attnT", "vnew"], [("ps", bO)])
            bS = bank()
            mm4(bS, lambda h: kdec[:, h, :], lambda h: vnew[:, h, :], ["kdec", "vnew"])
            tt("dve", Sm, Sm, bc4(cdb[:, t, :]), ALU.mult, ["Sm", "cdb"], ["Sm"])
            tt("dve", Sm, Sm, ps4(bS), ALU.add, ["Sm", ("ps", bS)], ["Sm"])
            cp("act", Sb, Sm, ["Sm"], ["Sb"])
            if gstop == 9:
                break
            cp("act", osb, ps4(bO), [("ps", bO)], ["osb"])
            for h in range(4):
                act(sqb[:, h, :], osb[:, h, :], AF.Square, ["osb"], ["sqb"], accum=gst[:, h:h + 1])
            P.add("act", lambda e: e.nop(), reads=["sqb"], writes=[("stat", 0)])
            P.add("act", lambda e: e.activation(out=gst[:, 4:8], in_=gst[:, 0:4], func=AF.Sqrt, bias=EPS,
                                                scale=1.0 / 128), reads=[("stat", 0)], writes=[("stat", 4)])
            P.add("dve", lambda e: e.reciprocal(out=gst[:, 4:8], in_=gst[:, 4:8]), reads=[("stat", 4)],
                  writes=[("stat", 4)])
            tt("dve", osb, osb, bc4(gst[:, 4:8]), ALU.mult, ["osb", ("stat", 4)], ["osb"])
            tt("pool", osb, osb, gnw, ALU.mult, ["osb", "gdnp"], ["osb"])
            act(sqb, ps4(bGG), AF.Silu, [("ps", bGG)], ["sqb"])
            tt("dve", ytok, osb, sqb, ALU.mult, ["osb", "sqb"], ["ytok"])
            if gstop == 10:
                break
            bY = bank()
            pyb = psb_bf(bY, 8)

            def trY(e):
                ins = None
                for h in range(4):
                    ins = e.transpose(pyb[:, h, :], ytok[:, h, :], identb[:])
                return ins
            pe(trY, ["ytok", "identb"], [("ps", bY)])
            cp("act", mixT[:, 4:8, tk], pyb[:, 0:4, :], [("ps", bY)], [("mixT", "g", t)])
            P.barrier()
            if gstop is not None and 100 <= gstop < 200 and t == gstop - 100:
                break
            if gstop is not None and gstop >= 300 and t_ == 3:
                break
        P.barrier()
        if stop == "M3":
            if l == 0:
                dump("mixT6", mixT[:, 2:8, :], [128, 6, S], ["mixT"])
            break

        t1r = vw(TMPA, 512)
        aqTm = v3(31232, 4, S, BF16)
        t2r = vw(TMPA + 512, 512)
        for qk in range(2):
            wv, wk = load_w(l, 5 + 1 + qk)
            dst = (aqT, akT)[qk]
            for c in range(2):
                for tt_ in range(4):
                    tok0 = tt_ * 512
                    b0, b1 = bank(), bank()
                    mm_feat(wv, wk, c, xT0, "xT", tok0, 512, b0)
                    mm_feat(wv, wk, 2 + c, xT0, "xT", tok0, 512, b1)
                    tt("dve", t1r, psb(b0), ropeC[:, tok0:tok0 + 512], ALU.mult, [("ps", b0), "ropeC"], ["t1r"])
                    tt("dve", t2r, psb(b1), ropeS[:, tok0:tok0 + 512], ALU.mult, [("ps", b1), "ropeS"], ["t2r"])
                    if qk == 1:
                        tt("pool", dst[:, c, tok0:tok0 + 512], t1r, t2r, ALU.add, ["t1r", "t2r"], [("aT", qk, c, tt_)])
                    else:
                        tt("pool", t1r, t1r, t2r, ALU.add, ["t1r", "t2r"], ["t1r"])
                        for hh in range(2):
                            ts("dve", aqTm[:, c * 2 + hh, tok0:tok0 + 512], t1r, cst[:, 896 + hh:897 + hh], ALU.mult,
                               ["t1r", "cst"], [("aT", qk, c, tt_, hh)])
        P.barrier()

        if stop == "A2":
            break
        acc = vw(0, 8192).rearrange("p (c k s) -> p c k s", c=2, k=2)
        vtok = vw(TMPA, 2048, BF16).rearrange("p (b c f) -> p b c f", b=16, c=2)
        PTb = [vw(TMPA + 2048 + i * 256, 256, BF16).rearrange("p (h k i) -> p h k i", h=2, k=2) for i in range(2)]
        pt_rr = 0
        for pi, dil in enumerate((1, 4, 16)):
            Ld = S // dil
            nb = Ld // 128

            def toks(r, n, dil=dil):
                s0 = n * 128 * dil + r
                return slice(s0, s0 + 127 * dil + 1, dil)
            for g4 in range(4):
                bV = bank()
                pvb = psb_bf(bV, 8)

                def trV(e, g4=g4, pvb=pvb, dil=dil, nb=nb):
                    ins = None
                    for bi in range(4):
                        blk = g4 * 4 + bi
                        r, n = blk // nb, blk % nb
                        for c in range(2):
                            ins = e.transpose(pvb[:, bi * 2 + c, :], avT[:, c, toks(r, n)], identb[:])
                    return ins
                pe(trV, ["avT", "identb"], [("ps", bV)])
                cp("act", vtok[:, g4 * 4:(g4 + 1) * 4, :, :].rearrange("p b c f -> p (b c) f"), pvb,
                   [("ps", bV)], [("vtok", g4)])
            for blk in range(16 if gstop != 500 else 0):
                r, n = blk // nb, blk % nb
                parts = (0, 1) if n > 0 else (1,)
                for c in range(2):
                    bS_ = bank()
                    sp_ = PS[:, bS_, :].rearrange("p (h k i) -> p h k i", h=2, k=2)

                    def smm(e, r=r, n=n, c=c, sp_=sp_, parts=parts):
                        ins = None
                        for hh in range(2):
                            rows = slice(hh * 64, (hh + 1) * 64)
                            for pa in parts:
                                kn = n - 1 if pa == 0 else n
                                ins = e.matmul(sp_[:, hh, pa, :], lhsT=akT[:, c, toks(r, kn)],
                                               rhs=aqTm[:, c * 2 + hh, toks(r, n)], start=True, stop=True)
                        return ins
                    pe(smm, ["aT"], [("ps", bS_)])
                    PT = PTb[pt_rr % 2]
                    ptk = ("PT", pt_rr % 2)
                    pt_rr += 1
                    ksl = slice(0, 2) if n > 0 else slice(1, 2)
                    act(PT[:, :, ksl, :], sp_[:, :, ksl, :], AF.Exp, [("ps", bS_)], [ptk], scale=0.125)
                    tt("dve", PT[:, :, ksl, :], PT[:, :, ksl, :], mask2[:, :, ksl, :], ALU.mult, [ptk, "mask2"], [ptk])
                    if gstop == 501:
                        continue
                    bO_ = bank()
                    op_ = PS[:, bO_, :].rearrange("p (q i) -> p q i", q=4)

                    def pvmm(e, r=r, n=n, c=c, op_=op_, PT=PT, parts=parts, nb=nb):
                        ins = None
                        for hh in range(2):
                            for idx, pa in enumerate(parts):
                                kb = r * nb + (n - 1 if pa == 0 else n)
                                st, sp2 = (idx == 0), (idx == len(parts) - 1)
                                e.matmul(op_[:, 2 * hh, :], lhsT=vtok[:, kb, c, :], rhs=PT[:, hh, pa, :], start=st, stop=sp2)
                            for idx, pa in enumerate(parts):
                                st, sp2 = (idx == 0), (idx == len(parts) - 1)
                                ins = e.matmul(op_[:, 2 * hh + 1, :], lhsT=onesb[:], rhs=PT[:, hh, pa, :], start=st, stop=sp2)
                        return ins
                    pe(pvmm, ["vtok", ptk, "onesb"], [("ps", bO_)])
                    for hh in range(2):
                        rows = slice(hh * 64, (hh + 1) * 64)
                        dsta = acc[rows, c, :, toks(r, n)]
                        srca = op_[rows, 2 * hh:2 * hh + 2, :]
                        ak_ = ("acc", c, hh, pi, blk)
                        if pi == 0:
                            cp("act", dsta, srca, [("ps", bO_)], [("acc", c, hh)])
                        else:
                            tt("dve", dsta, dsta, srca, ALU.add, [("acc", c, hh), ("ps", bO_)], [("acc", c, hh)])
        for c in range(2 if gstop not in (500, 501) else 0):
            P.add("dve", lambda e, c=c: e.reciprocal(out=acc[:, c, 1, :], in_=acc[:, c, 1, :]), reads=[("acc", c)],
                  writes=[("acc", c)])
            tt("dve", mixT[:, c, :], acc[:, c, 0, :], acc[:, c, 1, :], ALU.mult, [("acc", c)], [("mixT", c)])
        if l == 0:
            dump("mixT", mixT, [128, 8, S], ["mixT"])
        P.barrier()
        if stop == "M2":
            break

        set_wa(16384, 2)
        wo0, wo0k = load_w(l, 8)
        wo1, wo1k = load_w(l, 9)
        w_post = load_normw(l, 1)
        w_next = load_normw(l, 3)
        for t in range(NT):
            pb = bank2()

            def omm2(e, t=t, pb=pb):
                ins = None
                for half, wv in enumerate((wo0, wo1)):
                    for kc in range(8):
                        ins = e.matmul(PS[:, pb + half, :], lhsT=mixT[:, kc, t * 128:(t + 1) * 128], rhs=wv[:, kc, :],
                                       start=(kc == 0), stop=(kc == 7))
                return ins
            pe(omm2, ["mixT", wo0k, wo1k], [("ps", pb), ("ps", pb + 1)])
            epilogue(t, pb, src_h, h_d, w_post, w_next, xT0, "xT")
        P.barrier()
        if stop == "M4":
            break

        set_wa(16384, 2)
        oT = v3(8192, 8, S, BF16)
        qTx = v3(30720, 4, S, BF16)
        kTm = v3(34816, 8, MEM, BF16)
        vm = v3(35840, 2, D, BF16)
        memt = vw(22528, 1024)
        memn = vw(23552, 512, BF16)
        memT = v3(24064, 8, MEM, BF16)
        PTx = [vw(20480 + i * 512, 512, BF16).rearrange("p (m s) -> p m s", m=2) for i in range(2)]
        rden = vw(21504, 512)
        w_mem = load_normw(l, 2)
        for mt in range(2):
            dma("sp", memt, mem_d[mt * 128:(mt + 1) * 128, :], [], ["memt"])
            act(e_junk, memt, AF.Square, ["memt"], ["e_junk"], accum=stat[:, 16:17])
            P.add("act", lambda e: e.activation(out=stat[:, 17:18], in_=stat[:, 16:17], func=AF.Sqrt, bias=EPS,
                                                scale=1.0 / D), reads=["e_junk"], writes=[("stat", 16)])
            P.add("dve", lambda e: e.reciprocal(out=stat[:, 17:18], in_=stat[:, 17:18]), reads=[("stat", 16)],
                  writes=[("stat", 16)])
            stt("dve", memn, memt, stat[:, 17:18], normw[:, w_mem, :], ALU.mult, ALU.mult,
                ["memt", ("stat", 16), ("normw", w_mem)], ["memn"])
            b = bank()
            pt = psb_bf(b, 8)

            def trm(e, pt=pt):
                ins = None
                for kc in range(8):
                    ins = e.transpose(pt[:, kc, :], memn[:, kc * 128:(kc + 1) * 128], identb[:])
                return ins
            pe(trm, ["memn", "identb"], [("ps", b)])
            cp("act", memT[:, :, mt * 128:(mt + 1) * 128], pt, [("ps", b)], [("memT", mt)])
        for ti in range(2):
            wv, wk = load_w(l, 12 + ti)
            for j in range(4):
                b = bank()
                mm_feat(wv, wk, j, memT, "memT", 0, MEM, b)
                cp("act", kTm[:, ti * 4 + j, :], PS[:, b, 0:MEM], [("ps", b)], [("kTm", ti * 4 + j)])
        for ti in range(2):
            wv, wk = load_w(l, 14 + ti)
            for mt in range(2):
                b = bank()

                def vmm(e, wv=wv, mt=mt, b=b):
                    ins = None
                    for kc in range(8):
                        ins = e.matmul(PS[:, b, :], lhsT=memT[:, kc, mt * 128:(mt + 1) * 128], rhs=wv[:, kc, :],
                                       start=(kc == 0), stop=(kc == 7))
                    return ins
                pe(vmm, ["memT", wk], [("ps", b)])
                cp("act", vm[:, mt, ti * 512:(ti + 1) * 512], psb(b), [("ps", b)], [("vm", mt, ti)])
        px_rr = 0
        for g in range(2):
            wv, wk = load_w(l, 10 + g)
            for j in range(4):
                for tt_ in range(4):
                    b = bank()
                    mm_feat(wv, wk, j, xT0, "xT", tt_ * 512, 512, b)
                    cp("act", qTx[:, j, tt_ * 512:(tt_ + 1) * 512], psb(b), [("ps", b)], [("qTx", j, tt_)])
            for hh in range(2):
                h = 2 * g + hh
                for tt_ in range(4):
                    tok = slice(tt_ * 512, (tt_ + 1) * 512)
                    PT = PTx[px_rr % 2]
                    pk = ("PTx", px_rr % 2)
                    px_rr += 1
                    for mt in range(2):
                        b = bank()

                        def sx(e, b=b, mt=mt, h=h, hh=hh, tok=tok):
                            ins = None
                            for dc in range(2):
                                ins = e.matmul(PS[:, b, :], lhsT=kTm[:, 2 * h + dc, mt * 128:(mt + 1) * 128],
                                               rhs=qTx[:, 2 * hh + dc, tok], start=(dc == 0), stop=(dc == 1))
                            return ins
                        pe(sx, ["kTm", "qTx"], [("ps", b)])
                        act(PT[:, mt, :], psb(b), AF.Exp, [("ps", b)], [pk + (mt,)], scale=1.0 / 16)
                    bd = bank()

                    def dmm(e, bd=bd, PT=PT):
                        ins = None
                        for mt in range(2):
                            ins = e.matmul(PS[:, bd, :], lhsT=onesb[:], rhs=PT[:, mt, :], start=(mt == 0), stop=(mt == 1))
                        return ins
                    pe(dmm, [pk, "onesb"], [("ps", bd)])
                    P.add("dve", lambda e, bd=bd: e.reciprocal(out=rden, in_=psb(bd)), reads=[("ps", bd)], writes=["rden"])
                    for dc in range(2):
                        bo = bank()

                        def pvx(e, bo=bo, PT=PT, h=h, dc=dc):
                            ins = None
                            for mt in range(2):
                                ins = e.matmul(PS[:, bo, :], lhsT=vm[:, mt, h * 256 + dc * 128:h * 256 + (dc + 1) * 128],
                                               rhs=PT[:, mt, :], start=(mt == 0), stop=(mt == 1))
                            return ins
                        pe(pvx, [pk, "vm"], [("ps", bo)])
                        tt("dve", oT[:, 2 * h + dc, tok], psb(bo), rden, ALU.mult, [("ps", bo), "rden"],
                           [("oT", 2 * h + dc, tt_)])
        if l == 0:
            dump("oT", oT, [128, 8, S], ["oT"])
        P.barrier()
        if stop == "X1":
            break
        wo0, wo0k = load_w(l, 16)
        wo1, wo1k = load_w(l, 17)
        w_post = load_normw(l, 4)
        w_next = load_normw(l, 5)
        for t in range(NT):
            pb = bank2()

            def omm3(e, t=t, pb=pb, wo0=wo0, wo1=wo1):
                ins = None
                for half, wv in enumerate((wo0, wo1)):
                    for kc in range(8):
                        ins = e.matmul(PS[:, pb + half, :], lhsT=oT[:, kc, t * 128:(t + 1) * 128], rhs=wv[:, kc, :],
                                       start=(kc == 0), stop=(kc == 7))
                return ins
            pe(omm3, ["oT", wo0k, wo1k], [("ps", pb), ("ps", pb + 1)])
            epilogue(t, pb, h_d, h_d, w_post, w_next, xTF, "xTF")
        P.barrier()
        if stop == "X2":
            break

        set_wa(30720, 3)
        actT = v3(0, 22, S, BF16)
        gsl = [vw(36864 + i * 512, 512) for i in range(2)]
        gs_rr = 0
        for i in range(11):
            wv, wk = load_w(l, 18 + i)
            for tt_ in range(4):
                for jj in range(2):
                    bg, bu = bank(), bank()
                    mm_feat(wv, wk, jj, xTF, "xTF", tt_ * 512, 512, bg)
                    mm_feat(wv, wk, 2 + jj, xTF, "xTF", tt_ * 512, 512, bu)
                    gsb = gsl[gs_rr % 2]
                    gk_ = ("gs", gs_rr % 2)
                    gs_rr += 1
                    act(gsb, psb(bg), AF.Silu, [("ps", bg)], [gk_])
                    tt("dve", actT[:, 2 * i + jj, tt_ * 512:(tt_ + 1) * 512], gsb, psb(bu), ALU.mult,
                       [gk_, ("ps", bu)], [("actT", 2 * i + jj, tt_)])
        P.barrier()
        wd = v3(22528, 22, D, BF16)
        for q4 in range(4):
            k0, k1 = (0, 6) if q4 == 0 else (6 + (q4 - 1) * 6, 6 + q4 * 6) if q4 < 3 else (18, 22)
            dma("pool", wd[:, k0:k1, :], WD_d[l].rearrange("p (k n) -> p k n", k=22)[:, k0:k1, :], [], [("wd", q4)],
                prefetch=True)
        w_post = load_normw(l, 6)
        last = (l == nlayers - 1)
        for t in range(NT):
            pb = bank2()

            def dmm2(e, t=t, pb=pb):
                ins = None
                for half in range(2):
                    for kc in range(22):
                        ins = e.matmul(PS[:, pb + half, :], lhsT=actT[:, kc, t * 128:(t + 1) * 128],
                                       rhs=wd[:, kc, half * 512:(half + 1) * 512], start=(kc == 0), stop=(kc == 21))
                return ins
            pe(dmm2, ["actT", "wd"], [("ps", pb), ("ps", pb + 1)])
            o = epilogue(t, pb, h_d, out_d if last else h_d, w_post, None, None, None)
            if last:
                final_ops.append(o)
        P.barrier()

    if debug is not None:
        for k_, v_ in debug.items():
            if v_ is not None:
                final_ops.append(v_)
    stats = P.emit(final_wait_ops=final_ops)
    return nc, dbg_outs, stats


def host_consts():
    c = np.zeros((128, 1154), np.float32)
    j = np.arange(128)
    c[:, 0:128] = np.eye(128, dtype=np.float32)
    c[:, 128:256] = (j[:, None] <= j[None, :]).astype(np.float32)
    c[:, 256:384] = np.where(j[None, :] >= j[:, None], 0.0, -30000.0)
    c[:, 384:512] = 1.0 - np.eye(128, dtype=np.float32)
    c[:, 512:640] = 1.0
    c[:, 640:768] = (j[:, None] >= j[None, :]).astype(np.float32)
    c[:, 768:896] = (j[:, None] <= j[None, :]).astype(np.float32)
    inv_freq = np.float32(500000.0) ** (-(np.arange(0, 16, 2, dtype=np.float32)) / np.float32(16))
    d = j % 64
    c[:, 1152] = np.where(d < 8, inv_freq[d % 8], np.where(d < 16, inv_freq[d % 8], 0.0))
    c[:, 1153] = np.where(d < 8, -1.0, np.where(d < 16, 1.0, 0.0))
    c[:, 896] = (j < 64).astype(np.float32)
    c[:, 897] = (j >= 64).astype(np.float32)
    return c


def _tile(wcols):
    return np.ascontiguousarray(wcols.reshape(8, 128, 512).transpose(1, 0, 2)).reshape(128, 4096)


def host_weights(inp):
    W = np.zeros((L * NWT, 128, 4096), np.float32)
    WD = np.zeros((L, 128, 22 * 1024), np.float32)
    WAB = np.zeros((L, 128, 64), np.float32)
    NORMS = np.zeros((L * 7, D), np.float32)
    PSM = np.zeros((128, L * 54), np.float32)
    GDNP = np.zeros((L, 136), np.float32)
    sw = np.arange(256)
    dd = sw % 64
    swap = np.where(dd < 8, sw + 8, np.where(dd < 16, sw - 8, sw))
    for l in range(L):
        wi = inp["w_in"][l]
        aq, ak, av = wi[:, 0:256], wi[:, 256:512], wi[:, 512:768]
        cb, cc, cx = wi[:, 768:1024], wi[:, 1024:1280], wi[:, 1280:1536]
        gq, gk, gv = wi[:, 1536:2048], wi[:, 2048:2560], wi[:, 2560:3072]
        ga, gb, gg = wi[:, 3072:3076], wi[:, 3076:3080], wi[:, 3080:3592]
        tiles = []
        for c in range(2):
            s_ = slice(c * 128, (c + 1) * 128)
            tiles.append(np.concatenate([cx[:, s_], cc[:, s_], cb[:, s_], av[:, s_]], axis=1))
        tiles += [gq, gk, gv]
        tiles.append(gg)
        tiles.append(np.concatenate([aq, aq[:, swap]], axis=1))
        tiles.append(np.concatenate([ak, ak[:, swap]], axis=1))
        tiles += [inp["w_out"][l][:, 0:512], inp["w_out"][l][:, 512:1024]]
        tiles += [inp["w_xq"][l][:, 0:512], inp["w_xq"][l][:, 512:1024]]
        wkv = inp["w_xkv"][l]
        tiles += [wkv[:, 0:512], wkv[:, 512:1024], wkv[:, 1024:1536], wkv[:, 1536:2048]]
        tiles += [inp["w_xo"][l][:, 0:512], inp["w_xo"][l][:, 512:1024]]
        wgu = inp["w_gate_up"][l]
        for i in range(11):
            tiles.append(np.concatenate([wgu[:, 256 * i:256 * (i + 1)], wgu[:, FH + 256 * i:FH + 256 * (i + 1)]], axis=1))
        assert len(tiles) == NWT
        for i, tl in enumerate(tiles):
            W[l * NWT + i] = _tile(tl)
        WD[l] = np.ascontiguousarray(inp["w_down"][l].reshape(22, 128, D).transpose(1, 0, 2)).reshape(128, 22 * D)
        WAB[l] = np.ascontiguousarray(np.concatenate([ga, gb], axis=1).reshape(8, 128, 8).transpose(1, 0, 2)).reshape(128, 64)
        for i, nm in enumerate(["norm_mix_pre", "norm_mix_post", "norm_mem", "norm_xattn_pre", "norm_xattn_post",
                                "norm_ffn_pre", "norm_ffn_post"]):
            NORMS[l * 7 + i] = inp[nm][l]
        cs = inp["conv_short"][l]
        for c in range(2):
            for j in range(3):
                PSM[:, l * 54 + c * 3 + j] = cs[j, c * 128:(c + 1) * 128]
        cgd = inp["conv_gdn"][l]
        for cg in range(12):
            for j in range(4):
                PSM[:, l * 54 + 6 + cg * 4 + j] = cgd[j, cg * 128:(cg + 1) * 128]
        GDNP[l, 0:128] = inp["gdn_norm"][l]
        GDNP[l, 128:132] = inp["gdn_a_log"][l]
        GDNP[l, 132:136] = inp["gdn_dt_bias"][l]
    return dict(W=W, WD=WD, WAB=WAB, NORMS=NORMS, PSM=PSM, GDNP=GDNP, CONST=host_consts())


def make_in_maps(inp, cores, nlayers=L):
    inp = {k: np.asarray(v) for k, v in inp.items()}
    shared = host_weights(inp)
    shared["W"] = shared["W"][:nlayers * NWT]
    shared["WD"] = shared["WD"][:max(nlayers, 1)]
    maps = []
    for b in cores:
        m = dict(shared)
        m["x"] = np.ascontiguousarray(inp["x"][b], dtype=np.float32)
        m["mem"] = np.ascontiguousarray(inp["mem"][b], dtype=np.float32)
        m["pos"] = np.ascontiguousarray(inp["positions"][b:b + 1], dtype=np.int32)
        maps.append(m)
    return maps


def kernel(**inputs):
    nc, _, _ = build()
    maps = make_in_maps(inputs, list(range(8)))
    res = run_bass_kernel_spmd(nc, maps, core_ids=list(range(8)))
    return np.stack([np.asarray(r["out"], dtype=np.float32) for r in res.results], axis=0)
```
